# Optimizing a Trainium2 kernel written in Bass

```python
import math
import jax, jax.numpy as jnp
from jax import lax
import numpy as np

D_MODEL = 2048
BATCH = 2
SEQ = 16384
DEPTH = 4

D_MIX = D_MODEL
ATTN_WIDTH = D_MIX // 2
ATTN_HD = 64
N_Q_HEADS = ATTN_WIDTH // ATTN_HD
N_KV_HEADS = N_Q_HEADS // 8
GQA_GROUP = N_Q_HEADS // N_KV_HEADS
WINDOW = 128
BLOCK = 128
GLA_WIDTH = D_MIX - ATTN_WIDTH
GLA_HEADS = 4
GLA_DV = GLA_WIDTH // GLA_HEADS
GLA_DK = GLA_DV // 2
GLA_RANK = 16
GLA_TAU = 16.0
GLA_CHUNK = 64
D_FF = ((8 * D_MODEL // 3 + 255) // 256) * 256
N_MOD = 9
EPS = 1e-6

kernel_name = "hybrid_swa_sink_alibi_gla_macaron_adaln"


def _split_sizes():
    return [N_Q_HEADS * ATTN_HD, N_KV_HEADS * ATTN_HD, N_KV_HEADS * ATTN_HD,
            GLA_HEADS * GLA_DK, GLA_HEADS * GLA_DK, GLA_HEADS * GLA_DV,
            GLA_HEADS * GLA_DV, GLA_RANK]


def _rmsnorm_mod(x, w, shift, scale):
    xf = x.astype(jnp.float32)
    y = xf * lax.rsqrt(jnp.mean(xf * xf, axis=-1, keepdims=True) + EPS) * w.astype(jnp.float32)
    y = y * (1.0 + scale.astype(jnp.float32)) + shift.astype(jnp.float32)
    return y.astype(x.dtype)


def _swiglu(h, w1, w3, w2):
    return (jax.nn.silu(h @ w1) * (h @ w3)) @ w2


def _alibi_slopes(n):
    return jnp.exp2(-8.0 * jnp.arange(1, n + 1, dtype=jnp.float32) / n)


def _swa_sink_attention(q, k, v, sinks):
    B, S, _ = q.shape
    nb = S // BLOCK
    q = q.reshape(B, nb, BLOCK, N_KV_HEADS, GQA_GROUP, ATTN_HD)

    def slab(t):
        t = t.reshape(B, S, N_KV_HEADS, ATTN_HD)
        t = jnp.pad(t, ((0, 0), (BLOCK, 0), (0, 0), (0, 0)))
        t = t.reshape(B, nb + 1, BLOCK, N_KV_HEADS, ATTN_HD)
        return jnp.concatenate([t[:, :-1], t[:, 1:]], axis=2)

    ks, vs = slab(k), slab(v)
    s = jnp.einsum('bnqkgd,bnskd->bnkgqs', q, ks).astype(jnp.float32) * (ATTN_HD ** -0.5)
    qi = jnp.arange(BLOCK)[:, None]
    kj = jnp.arange(2 * BLOCK)[None, :]
    dist = qi + BLOCK - kj
    band = (dist >= 0) & (dist < WINDOW)
    valid = band[None] & ((jnp.arange(nb) > 0)[:, None, None] | (kj >= BLOCK)[None])
    slopes = _alibi_slopes(N_Q_HEADS).reshape(N_KV_HEADS, GQA_GROUP)
    bias = -slopes[:, :, None, None] * dist.astype(jnp.float32)[None, None]
    s = jnp.where(valid[None, :, None, None], s + bias[None, None], -jnp.inf)
    sink = sinks.astype(jnp.float32).reshape(N_KV_HEADS, GQA_GROUP)[None, None, :, :, None, None]
    m = jnp.maximum(jnp.max(s, axis=-1, keepdims=True), sink)
    p = jnp.exp(s - m)
    p = p / (jnp.sum(p, axis=-1, keepdims=True) + jnp.exp(sink - m))
    o = jnp.einsum('bnkgqs,bnskd->bnqkgd', p.astype(vs.dtype), vs)
    return o.reshape(B, S, N_Q_HEADS * ATTN_HD)


def _gla(q, k, v, g, r, norm_w):
    B, S, _ = q.shape
    N = S // GLA_CHUNK
    out_dtype = v.dtype

    def chunks(t, d):
        return t.reshape(B, N, GLA_CHUNK, GLA_HEADS, d).transpose(0, 3, 1, 2, 4).astype(jnp.float32)

    q = chunks(q, GLA_DK) * (GLA_DK ** -0.5)
    k = chunks(k, GLA_DK)
    v = chunks(v, GLA_DV)
    b = jnp.cumsum(chunks(g, GLA_DK), axis=3)
    q_t = q * jnp.exp(b)
    k_t = k * jnp.exp(-b)
    causal = jnp.tril(jnp.ones((GLA_CHUNK, GLA_CHUNK), dtype=bool))
    a = jnp.where(causal, jnp.einsum('bhncd,bhnsd->bhncs', q_t, k_t), 0.0)
    o_intra = jnp.einsum('bhncs,bhnse->bhnce', a, v)
    b_last = b[:, :, :, -1:, :]
    k_end = k * jnp.exp(b_last - b)
    decay = jnp.exp(b_last[:, :, :, 0, :])

    def step(state, inp):
        qc, kc, vc, dc = inp
        o = jnp.einsum('bhcd,bhde->bhce', qc, state)
        state = dc[..., None] * state + jnp.einsum('bhcd,bhce->bhde', kc, vc)
        return state, o

    xs = (jnp.moveaxis(q_t, 2, 0), jnp.moveaxis(k_end, 2, 0), jnp.moveaxis(v, 2, 0), jnp.moveaxis(decay, 2, 0))
    s0 = jnp.zeros((B, GLA_HEADS, GLA_DK, GLA_DV), jnp.float32)
    _, o_inter = lax.scan(step, s0, xs)
    o = o_intra + jnp.moveaxis(o_inter, 0, 2)
    o = o.transpose(0, 2, 3, 1, 4).reshape(B, S, GLA_HEADS, GLA_DV)
    o = o * lax.rsqrt(jnp.mean(o * o, axis=-1, keepdims=True) + EPS) * norm_w.astype(jnp.float32)
    o = o.reshape(B, S, GLA_HEADS * GLA_DV) * jax.nn.silu(r.astype(jnp.float32))
    return o.astype(out_dtype)


def _mixer(h, w_in, gate_w2, gate_b, sinks, gla_norm, w_out):
    p = h @ w_in
    idx = list(np.cumsum(_split_sizes())[:-1])
    q_a, k_a, v_a, q_g, k_g, v_g, r_g, gate_lr = jnp.split(p, idx, axis=-1)
    g = jax.nn.log_sigmoid((gate_lr @ gate_w2 + gate_b).astype(jnp.float32)) / GLA_TAU
    o_attn = _swa_sink_attention(q_a, k_a, v_a, sinks)
    o_gla = _gla(q_g, k_g, v_g, g, r_g, gla_norm)
    return jnp.concatenate([o_attn, o_gla.astype(o_attn.dtype)], axis=-1) @ w_out


def setup_inputs(seed: int = 0) -> dict:
    key = jax.random.key(seed)
    ks = jax.random.split(key, 24)
    nrm = jax.random.normal
    f32 = jnp.float32
    n_in = sum(_split_sizes())
    L, D = DEPTH, D_MODEL
    return {
        "x": nrm(ks[0], (BATCH, SEQ, D), f32),
        "c": nrm(ks[1], (BATCH, D), f32),
        "ada_w": nrm(ks[2], (L, D, N_MOD * D), f32) * D ** -0.5,
        "ada_b": nrm(ks[3], (L, N_MOD * D), f32) * 0.01,
        "norm_ffn1": 1.0 + 0.05 * nrm(ks[4], (L, D), f32),
        "ffn1_w1": nrm(ks[5], (L, D, D_FF), f32) * D ** -0.5,
        "ffn1_w3": nrm(ks[6], (L, D, D_FF), f32) * D ** -0.5,
        "ffn1_w2": nrm(ks[7], (L, D_FF, D), f32) * D_FF ** -0.5,
        "norm_mix": 1.0 + 0.05 * nrm(ks[8], (L, D), f32),
        "w_in": nrm(ks[9], (L, D, n_in), f32) * D ** -0.5,
        "gla_gate_w2": nrm(ks[10], (L, GLA_RANK, GLA_HEADS * GLA_DK), f32) * GLA_RANK ** -0.5,
        "gla_gate_b": nrm(ks[11], (L, GLA_HEADS * GLA_DK), f32) * 0.01,
        "attn_sinks": nrm(ks[12], (L, N_Q_HEADS), f32) * 0.5,
        "gla_norm": 1.0 + 0.05 * nrm(ks[13], (L, GLA_DV), f32),
        "w_out": nrm(ks[14], (L, D_MIX, D), f32) * D_MIX ** -0.5,
        "norm_ffn2": 1.0 + 0.05 * nrm(ks[15], (L, D), f32),
        "ffn2_w1": nrm(ks[16], (L, D, D_FF), f32) * D ** -0.5,
        "ffn2_w3": nrm(ks[17], (L, D, D_FF), f32) * D ** -0.5,
        "ffn2_w2": nrm(ks[18], (L, D_FF, D), f32) * D_FF ** -0.5,
        "final_ada_w": nrm(ks[19], (D, 2 * D), f32) * D ** -0.5,
        "final_ada_b": nrm(ks[20], (2 * D,), f32) * 0.01,
        "final_norm": 1.0 + 0.05 * nrm(ks[21], (D,), f32),
    }


def reference(x, c, ada_w, ada_b, norm_ffn1, ffn1_w1, ffn1_w3, ffn1_w2, norm_mix, w_in,
              gla_gate_w2, gla_gate_b, attn_sinks, gla_norm, w_out, norm_ffn2, ffn2_w1,
              ffn2_w3, ffn2_w2, final_ada_w, final_ada_b, final_norm):
    B = x.shape[0]
    c_act = jax.nn.silu(c)
    h = x
    for l in range(DEPTH):
        mod = (c_act @ ada_w[l] + ada_b[l]).reshape(B, N_MOD, D_MODEL)[:, :, None, :]
        sh1, sc1, g1, sh2, sc2, g2, sh3, sc3, g3 = [mod[:, i] for i in range(N_MOD)]
        u = _rmsnorm_mod(h, norm_ffn1[l], sh1, sc1)
        h = h + 0.5 * g1 * _swiglu(u, ffn1_w1[l], ffn1_w3[l], ffn1_w2[l])
        u = _rmsnorm_mod(h, norm_mix[l], sh2, sc2)
        h = h + g2 * _mixer(u, w_in[l], gla_gate_w2[l], gla_gate_b[l], attn_sinks[l], gla_norm[l], w_out[l])
        u = _rmsnorm_mod(h, norm_ffn2[l], sh3, sc3)
        h = h + 0.5 * g3 * _swiglu(u, ffn2_w1[l], ffn2_w3[l], ffn2_w2[l])
    fmod = (c_act @ final_ada_w + final_ada_b).reshape(B, 2, D_MODEL)[:, :, None, :]
    return _rmsnorm_mod(h, final_norm, fmod[:, 0], fmod[:, 1])
```

```python
import contextlib
import numpy as np
import concourse.bass as bass
import concourse.mybir as mybir
from concourse.bass_utils import run_bass_kernel_spmd

F32 = mybir.dt.float32
BF16 = mybir.dt.bfloat16
AF = mybir.ActivationFunctionType
ALU = mybir.AluOpType

COMPUTE = ("pe", "act", "dve", "pool")
ALLENG = ("pe", "act", "dve", "pool", "sp")


class Op:
    __slots__ = ("eng", "emit", "waits", "idx", "milestone", "dma_sem", "dma_val", "count", "dma_inc")

    def __init__(self, eng, emit):
        self.eng = eng
        self.emit = emit
        self.waits = []
        self.idx = -1
        self.milestone = False
        self.dma_sem = None
        self.dma_val = 0
        self.dma_inc = 16
        self.count = 0


class Prog:
    def __init__(self, nc, same_engine_sync=True):
        self.nc = nc
        self.es = contextlib.ExitStack()
        self.streams = {e: [] for e in ALLENG}
        self.last_writer = {}
        self.readers = {}
        self.known = {e: {f: -1 for f in COMPUTE} for e in ALLENG}
        self.known_dma = {e: {} for e in ALLENG}
        self.dma_count = {}
        self.same_engine_sync = same_engine_sync
        self.buf_range = {}
        self.buf_keys = {}
        self.buf_ovl = {}
        self.pf = ""
        self.local = set()

    def kmap(self, k):
        i = k.find(":")
        base = k[:i] if i >= 0 else k
        return self.pf + k if base in self.local else k

    def sb(self, name, shape, dt):
        return self.es.enter_context(self.nc.sbuf_tensor("sb_" + name, list(shape), dt))

    def ps(self, name, shape, dt=F32):
        return self.es.enter_context(self.nc.psum_tensor(name, list(shape), dt))

    def region(self, buf, lo, hi):
        if buf in self.buf_range:
            assert self.buf_range[buf] == (lo, hi), buf
            return
        self.buf_range[buf] = (lo, hi)
        self.buf_keys[buf] = set()
        self.buf_ovl[buf] = []
        for b2, (l2, h2) in self.buf_range.items():
            if b2 != buf and lo < h2 and l2 < hi:
                self.buf_ovl[buf].append(b2)
                self.buf_ovl[b2].append(buf)

    def _expand(self, key):
        i = key.find(":")
        buf = key[:i] if i >= 0 else key
        if buf not in self.buf_range:
            return (key,)
        self.buf_keys[buf].add(key)
        out = [key]
        for b2 in self.buf_ovl[buf]:
            out.extend(self.buf_keys[b2])
        return out

    def _deps(self, op, reads, writes):
        E = op.eng
        reads = [self.kmap(k) for k in reads]
        writes = [self.kmap(k) for k in writes]
        deps = []
        for r in reads:
            for k in self._expand(r):
                w = self.last_writer.get(k)
                if w is not None:
                    deps.append(w)
        for w in writes:
            for k in self._expand(w):
                lw = self.last_writer.get(k)
                if lw is not None:
                    deps.append(lw)
                deps.extend(self.readers.get(k, ()))
        best = {}
        for d in deps:
            if d is op:
                continue
            if d.dma_sem is not None:
                k = self.known_dma[E].get(d.dma_sem, 0)
                if d.dma_val > k:
                    self.known_dma[E][d.dma_sem] = d.dma_val
                    op.waits.append(("dma", d.dma_sem, d.dma_val))
                continue
            F = d.eng
            if F == E and (E == "pe" or not self.same_engine_sync):
                continue
            if d.idx <= self.known[E][F]:
                continue
            if F not in best or best[F].idx < d.idx:
                best[F] = d
        for F, d in best.items():
            self.known[E][F] = d.idx
            d.milestone = True
            op.waits.append(("eng", F, d))
        for r in reads:
            self.readers.setdefault(r, []).append(op)
        for w in writes:
            self.last_writer[w] = op
            self.readers[w] = []

    def op(self, eng, emit, reads=(), writes=()):
        o = Op(eng, emit)
        o.idx = len(self.streams[eng])
        self._deps(o, reads, writes)
        self.streams[eng].append(o)
        return o

    def dma(self, queue, out, in_, reads=(), writes=(), sem=None, **kw):
        def emit(eng, out=out, in_=in_, kw=kw):
            return eng.dma_start(out=out, in_=in_, **kw)
        o = Op(queue, emit)
        o.idx = len(self.streams[queue])
        self._deps(o, reads, writes)
        self.dma_count[sem] = self.dma_count.get(sem, 0) + 16
        o.dma_sem = sem
        o.dma_val = self.dma_count[sem]
        self.streams[queue].append(o)
        return o

    def cc(self, emit, reads=(), writes=(), sem=None):
        reads = list(reads) + ["__ccchain"]
        writes = list(writes) + ["__ccchain"]
        o = Op("pool", emit)
        o.idx = len(self.streams["pool"])
        self._deps(o, reads, writes)
        self.dma_count[sem] = self.dma_count.get(sem, 0) + 1
        o.dma_sem = sem
        o.dma_val = self.dma_count[sem]
        o.dma_inc = 1
        self.streams["pool"].append(o)
        return o

    def build(self, final_waits=()):
        nc = self.nc
        es = self.es
        esem = {e: es.enter_context(nc.semaphore("sem_" + e)) for e in COMPUTE}
        dsem = {k: es.enter_context(nc.semaphore("dsem_" + str(k))) for k in self.dma_count}
        for e in COMPUTE:
            c = 0
            for o in self.streams[e]:
                if o.milestone:
                    c += 1
                o.count = c
        block = es.enter_context(nc.Block())

        def run(engname):
            def f(eng):
                for o in self.streams[engname]:
                    for w in o.waits:
                        if w[0] == "dma":
                            eng.wait_ge(dsem[w[1]], w[2])
                        else:
                            eng.wait_ge(esem[w[1]], w[2].count)
                    ins = o.emit(eng)
                    if o.dma_sem is not None:
                        ins.then_inc(dsem[o.dma_sem], o.dma_inc)
                    elif o.milestone:
                        ins.then_inc(esem[engname], 1)
                if engname == "sp":
                    for k in final_waits:
                        eng.wait_ge(dsem[k], self.dma_count[k])
            return f

        block.tensor(run("pe"))
        block.scalar(run("act"))
        block.vector(run("dve"))
        block.gpsimd(run("pool"))
        block.sync(run("sp"))
        es.close()


D = 2048
NCH = 16
T = 512
NQH, HD = 16, 64
GH, GDK, GDV = 4, 128, 256
EPS = 1e-6
NCORE = 8
WB = 8192
NFM = 26
NAD = 18
NINFO = 32
SW = 1424


class Cfg:
    def __init__(self, L=4, NT=4096, FC=44, solo=False):
        self.L, self.NT, self.FC = L, NT, FC
        self.solo = solo
        self.ncore = 2 if solo else NCORE
        self.S13 = (FC + 7) // 8
        self.NB13 = (self.S13 + 1) // 2
        self.NBF = self.NB13 + 2
        self.NBLK = 2 * self.NBF + 3
        self.NTILE = NT // T
        self.NADT = L * NAD + 4


def build_program(cfg):
    L, NT, FC, NBLK, NTILE = cfg.L, cfg.NT, cfg.FC, cfg.NBLK, cfg.NTILE
    NADT = cfg.NADT
    nc = bass.Bass("TRN2", target_bir_lowering=False)
    dt_in = lambda n, s: nc.dram_tensor(n, list(s), F32, kind="ExternalInput").ap()
    xT = dt_in("xT", [NCH * 128, NT])
    NRW = NCORE if cfg.solo else 1
    wsh = dt_in("wsh", [L * NRW * NBLK * 128, WB])
    adaw = dt_in("adaw", [NRW * NADT * 128, 2048])
    adab = dt_in("adab", [128, NRW * NADT])
    cT = dt_in("cT", [128, 512])
    normw = dt_in("normw", [128, (3 * L + 1) * 16])
    gw2 = dt_in("gw2", [16, L * 512])
    gbias = dt_in("gbias", [1, L * 512])
    sinkT = dt_in("sinkT", [128, L * 8])
    glan = dt_in("glan", [128, L * 2])
    cinfo = dt_in("cinfo", [128, NINFO])
    ebias_d = dt_in("ebias", [128, 4096])
    umat_d = dt_in("umat", [128, 128])
    gmask_d = dt_in("gmask", [128, 128])
    yT = nc.dram_tensor("yT", [NCH * 128, NT], F32, kind="ExternalOutput").ap()

    wbounce = [nc.dram_tensor("wbounce%d" % l, [NBLK * 128, WB], BF16) for l in range(L)]
    wg = [nc.dram_tensor("wg%d" % l, [NCORE * NBLK * 128, WB], BF16) for l in range(L)]
    hscr = nc.dram_tensor("hscr", [NCH * 128, NT], F32).ap()
    modb = nc.dram_tensor("modb", [128, NADT * 2], F32)
    modg = nc.dram_tensor("modg", [NCORE * 128, NADT * 2], F32)
    stb = [nc.dram_tensor("stb%d" % l, [128, SW], F32) for l in range(L)]
    stg = [nc.dram_tensor("stg%d" % l, [NCORE * 128, SW], F32) for l in range(L)]

    P = Prog(nc)
    RG = [list(range(NCORE))]

    hbuf = P.sb("hbuf", [128, NCH, T], F32)
    ubuf = P.sb("ubuf", [128, NCH, T], BF16)
    wring = [P.sb("wring%d" % i, [128, WB], BF16) for i in range(3)]
    onesD = P.sb("onesD", [128, 128], BF16)
    ones256 = P.sb("ones256", [128, 128], F32)
    oneg = P.sb("oneg", [128, 2, 128], BF16)
    umat = P.sb("umat", [128, 128], F32)
    gmask = P.sb("gmask", [128, 128], F32)
    ebias = P.sb("ebias", [128, 4096], BF16)
    info = P.sb("info", [128, NINFO], F32)
    epsb = P.sb("epsb", [128, 1], F32)
    nwt = P.sb("nwt", [128, (3 * L + 1) * 16], F32)
    modL = P.sb("modL", [128, L * 144 + 32], F32)
    vecs = P.sb("vecs", [128, (L * 9 + 2) * 16], F32)
    gw2s = P.sb("gw2s", [17, L * 512], F32)
    esink = P.sb("esink", [128, L * 8], F32)
    glans = P.sb("glans", [128, L * 2], F32)
    ARENA_F32 = 22016
    arena = P.sb("arena", [128, ARENA_F32], F32)

    class Arena:
        def __init__(self):
            self.off = 0
            self.pf = ""
        def reset(self, pf):
            self.off = 0
            self.pf = pf
            P.pf = pf
            P.local = set()
        def seek(self, off):
            self.off = off
        def alloc(self, name, nbytes, dt, shape=None):
            nb = (nbytes + 31) // 32 * 32
            lo = self.off
            self.off += nb
            assert self.off <= ARENA_F32 * 4, (name, self.off)
            P.local.add(name)
            P.region(self.pf + name, lo, lo + nb)
            v = arena[:, lo // 4:(lo + nb) // 4]
            if dt == BF16:
                v = v.bitcast(BF16)[:, 0:nbytes // 2]
            else:
                v = v[:, 0:nbytes // 4]
            return v
    A = Arena()

    pb = [P.ps("pb%d" % i, [128, 512]) for i in range(8)]

    def PB(i):
        return pb[i][:]

    def PB3(i, t=128):
        return pb[i][:].rearrange("p (c t) -> p c t", t=t)

    cnt = {"ring": 0, "cast": 0, "eng": 0}

    A.reset("p_")
    castf = [A.alloc("castf%d" % i, 16384, F32) for i in range(2)]
    castb = [A.alloc("castb%d" % i, 8192, BF16) for i in range(2)]
    ebtmp = A.alloc("ebtmp", 16384, F32)

    P.dma("sp", info[:], cinfo, writes=["info"], sem="k1")
    P.dma("sp", umat[:], umat_d, writes=["umat"], sem="k2")
    P.dma("sp", gmask[:], gmask_d, writes=["gmask"], sem="k3")
    P.dma("sp", nwt[:], normw, writes=["nwt"], sem="k4")
    P.dma("sp", gw2s[0:16, :], gw2, writes=["gw2s"], sem="k5")
    P.dma("sp", gw2s[16:17, :], gbias, writes=["gbs"], sem="k6")
    P.dma("sp", esink[:], sinkT, writes=["esink"], sem="k7")
    P.dma("sp", glans[:], glan, writes=["glans"], sem="k8")
    P.dma("sp", ebtmp, ebias_d, writes=["ebtmp"], sem="k9")
    P.op("dve", lambda e: e.tensor_copy(ebias[:], ebtmp), reads=["ebtmp"], writes=["ebias"])
    P.op("act", lambda e: e.activation(esink[:], esink[:], AF.Exp), reads=["esink"], writes=["esink"])
    P.op("pool", lambda e: e.memset(onesD[:], 1.0 / D), writes=["onesD"])
    P.op("pool", lambda e: e.memset(ones256[:], 1.0 / GDV), writes=["ones256"])
    P.op("pool", lambda e: e.memset(epsb[:], EPS), writes=["epsb"])
    P.op("pool", lambda e: e.memset(oneg[:], 0.0), writes=["oneg"])
    P.op("pool", lambda e: e.memset(oneg[:, 0, 0:64], 1.0), reads=["oneg"], writes=["oneg"])
    P.op("pool", lambda e: e.memset(oneg[:, 1, 64:128], 1.0), reads=["oneg"], writes=["oneg"])

    solo = cfg.solo
    NR = NCORE if solo else 1
    cast_engs = ("dve", "pool")
    wgkeys = {}
    for l in range(L if not getattr(cfg, "skip_cast", False) else 0):
        wbkeys = []
        for rr in range(NR):
            for b in range(NBLK):
                for hf in range(2):
                    i = cnt["cast"] % 2
                    cnt["cast"] += 1
                    r0 = ((l * NR + rr) * NBLK + b) * 128
                    P.dma("sp", castf[i], wsh[r0:r0 + 128, hf * 4096:(hf + 1) * 4096],
                          writes=["castf%d" % i], sem="cf%d" % i)
                    ce = cast_engs[cnt["cast"] % 2]
                    P.op(ce, lambda e, i=i: e.tensor_copy(castb[i], castf[i]), reads=["castf%d" % i], writes=["castb%d" % i])
                    if solo:
                        wk = "wgb%d:%d:%d:%d" % (l, rr, b, hf)
                        d0 = (rr * NBLK + b) * 128
                        P.dma("act", wg[l].ap()[d0:d0 + 128, hf * 4096:(hf + 1) * 4096], castb[i],
                              reads=["castb%d" % i], writes=[wk], sem="cb%d" % i)
                    else:
                        wk = "wb%d:%d:%d" % (l, b, hf)
                        wbkeys.append(wk)
                        P.dma("act", wbounce[l].ap()[b * 128:(b + 1) * 128, hf * 4096:(hf + 1) * 4096], castb[i],
                              reads=["castb%d" % i], writes=[wk], sem="cb%d" % i)
        if not solo:
            P.cc(lambda e, l=l: e.collective_compute("AllGather", ALU.bypass, replica_groups=RG,
                                                      ins=[wbounce[l].ap().opt()], outs=[wg[l].ap().opt()]),
                 reads=wbkeys, writes=["wg%d" % l], sem="ccw%d" % l)

    def wkeys(l, rank, b):
        if solo:
            return ["wgb%d:%d:%d:%d" % (l, rank, b, hf) for hf in range(2)]
        return ["wg%d" % l]

    if not getattr(cfg, 'skip_ada', False):
        NCHK = NR * NADT
        cTs = A.alloc("cTs", 2048, F32)
        adb = A.alloc("adb", NCHK * 4, F32)
        modp = A.alloc("modp", NADT * 8, F32)
        modall = A.alloc("modall", NCORE * NADT * 8, F32)
        modown = A.alloc("modown", NCORE * NADT * 4, F32)
        modtmp = A.alloc("modtmp", NCORE * NADT * 4, F32)
        P.dma("sp", cTs, cT, writes=["cTs"], sem="k10")
        P.dma("sp", adb, adab, writes=["adb"], sem="k11")
        P.op("act", lambda e: e.activation(cTs, cTs, AF.Silu), reads=["cTs"], writes=["cTs"])
        cT3 = cTs.rearrange("p (k b) -> p k b", b=32)
        ma3 = modall.rearrange("p (r c b) -> p r c b", r=NCORE, b=2)
        modp3 = modp.rearrange("p (c b) -> p c b", b=2)
        GRP = 16
        gcount = 0
        for rr in range(NR):
            for g0 in range(0, NADT, GRP):
                ng = min(GRP, NADT - g0)
                bank = 7 if gcount % 2 == 0 else 5
                gcount += 1
                psg = pb[bank][:].rearrange("p (c x) -> p c x", x=32)
                for ch0 in range(g0, g0 + ng, 2):
                    i = cnt["cast"] % 2
                    cnt["cast"] += 1
                    n = min(2, g0 + ng - ch0)
                    gch = rr * NADT + ch0
                    src = adaw[gch * 128:(gch + n) * 128, :].rearrange("(c p) x -> p c x", p=128)
                    dst = castf[i][:, 0:n * 2048].rearrange("p (c x) -> p c x", x=2048)
                    P.dma("sp", dst, src, writes=["castf%d" % i], sem="cf%d" % i)
                    for k in range(n):
                        def mm(e, i=i, k=k, cl=ch0 + k - g0, psg=psg):
                            ins = None
                            for kc in range(16):
                                ins = e.matmul(psg[:, cl, :], castf[i][:, k * 2048 + kc * 128:k * 2048 + (kc + 1) * 128],
                                               cT3[:, kc, :], start=(kc == 0), stop=(kc == 15))
                            return ins
                        P.op("pe", mm, reads=["castf%d" % i, "cTs"], writes=["pb%d" % bank])
                for b in range(2):
                    o_ap = ma3[:, rr, g0:g0 + ng, b] if solo else modp3[:, g0:g0 + ng, b]
                    okey = "modall" if solo else "modp"
                    P.op("dve", lambda e, o_ap=o_ap, psg=psg, ng=ng, b=b, c0=rr * NADT + g0: e.tensor_tensor(o_ap, psg[:, 0:ng, b], adb[:, c0:c0 + ng], ALU.add),
                         reads=["pb%d" % bank, "adb", okey], writes=[okey])
        if not solo:
            P.dma("sp", modb.ap(), modp, reads=["modp"], writes=["modb"], sem="k12")
            P.cc(lambda e: e.collective_compute("AllGather", ALU.bypass, replica_groups=RG,
                                                ins=[modb.ap().opt()], outs=[modg.ap().opt()]),
                 reads=["modb"], writes=["modg"], sem="ccm")
            P.dma("sp", modall.rearrange("p (r x) -> p r x", r=NCORE), modg.ap().rearrange("(r p) x -> p r x", p=128),
                  reads=["modg"], writes=["modall"], sem="k13")
        ma4 = modall.rearrange("p (x b) -> p x b", b=2)
        P.op("dve", lambda e: e.tensor_scalar(modtmp, ma4[:, :, 0], info[:, 0:1], None, ALU.mult),
             reads=["modall", "info"], writes=["modtmp"])
        P.op("dve", lambda e: e.scalar_tensor_tensor(modown, ma4[:, :, 1], info[:, 1:2], modtmp, ALU.mult, ALU.add),
             reads=["modall", "info", "modtmp"], writes=["modown"])
        mo3 = modown.rearrange("p (r x) -> p r x", r=NCORE)
        for l in range(L):
            for r in range(NCORE):
                P.op("pool", lambda e, l=l, r=r: e.tensor_copy(modL[:, l * 144 + r * NAD:l * 144 + (r + 1) * NAD],
                                                                mo3[:, r, l * NAD:(l + 1) * NAD]),
                     reads=["modown"], writes=["modL"])
        for r in range(NCORE):
            P.op("pool", lambda e, r=r: e.tensor_copy(modL[:, L * 144 + r * 4:L * 144 + (r + 1) * 4],
                                                       mo3[:, r, L * NAD:L * NAD + 4]),
                 reads=["modown"], writes=["modL"])

    def mvec(l, i):
        return modL[:, l * 144 + i * 16:l * 144 + (i + 1) * 16]

    def vec(l, k):
        o = (l * 9 + k) * 16
        return vecs[:, o:o + 16]

    for l in range(L):
        for s in range(3):
            nw = nwt[:, (l * 3 + s) * 16:(l * 3 + s + 1) * 16]
            P.op("dve", lambda e, l=l, s=s, nw=nw: e.scalar_tensor_tensor(vec(l, s * 3), mvec(l, s * 3 + 1), 1.0, nw, ALU.add, ALU.mult),
                 reads=["modL", "nwt"], writes=["vecs"])
            P.op("dve", lambda e, l=l, s=s: e.tensor_copy(vec(l, s * 3 + 1), mvec(l, s * 3)), reads=["modL"], writes=["vecs"])
            gs = 1.0 if s == 1 else 0.5
            P.op("dve", lambda e, l=l, s=s, gs=gs: e.tensor_scalar(vec(l, s * 3 + 2), mvec(l, s * 3 + 2), gs, None, ALU.mult),
                 reads=["modL"], writes=["vecs"])
    nwf = nwt[:, 3 * L * 16:(3 * L + 1) * 16]
    P.op("dve", lambda e: e.scalar_tensor_tensor(vec(L, 0), modL[:, L * 144 + 16:L * 144 + 32], 1.0, nwf, ALU.add, ALU.mult),
         reads=["modL", "nwt"], writes=["vecs"])
    P.op("dve", lambda e: e.tensor_copy(vec(L, 1), modL[:, L * 144:L * 144 + 16]), reads=["modL"], writes=["vecs"])

    def load_tile(src, t, srckey):
        s3 = src.rearrange("(c p) n -> p c n", p=128)
        for c0 in range(0, NCH, 4):
            P.dma("pool", hbuf[:, c0:c0 + 4, :], s3[:, c0:c0 + 4, t * T:(t + 1) * T],
                  reads=[srckey + ":%d" % t], writes=["h:%d" % c for c in range(c0, c0 + 4)], sem="ldh%d" % (c0 // 4))

    def store_tile(dst, t, dstkey, sem="sth"):
        d3 = dst.rearrange("(c p) n -> p c n", p=128)
        for c0 in range(0, NCH, 4):
            P.dma("pool", d3[:, c0:c0 + 4, t * T:(t + 1) * T], hbuf[:, c0:c0 + 4, :],
                  reads=["h:%d" % c for c in range(c0, c0 + 4)], writes=[dstkey + ":%d" % t], sem=sem)

    def ring_load(l, src, width=WB):
        src_ap, keys = src
        i = cnt["ring"] % 3
        cnt["ring"] += 1
        P.dma("sp", wring[i][:, 0:width], src_ap, reads=keys, writes=["wr%d" % i], sem="wr%d" % i)
        return wring[i], "wr%d" % i

    def wblock(l, rank, b, width=WB):
        r0 = (rank * NBLK + b) * 128
        return wg[l].ap()[r0:r0 + 128, 0:width], wkeys(l, rank, b)

    def rmsnorm(sq, rstd, nt1, Av, Bv, out_bf=True):
        for c in range(NCH):
            s = sq[c % 4]
            P.op("act", lambda e, s=s, c=c: e.activation(s, hbuf[:, c, :], AF.Square), reads=["h:%d" % c], writes=["sq%d" % (c % 4)])
            P.op("pe", lambda e, s=s, c=c: e.matmul(PB(6), onesD[:], s, start=(c == 0), stop=(c == NCH - 1)),
                 reads=["sq%d" % (c % 4), "onesD"], writes=["pb6"])
        P.op("act", lambda e: e.activation(rstd, PB(6), AF.Ln, bias=epsb[:, 0:1]), reads=["pb6", "epsb"], writes=["rstd"])
        P.op("act", lambda e: e.activation(rstd, rstd, AF.Exp, scale=-0.5), reads=["rstd"], writes=["rstd"])
        for c in range(NCH):
            n1 = nt1[c % 2]
            P.op("dve", lambda e, c=c, n1=n1: e.scalar_tensor_tensor(n1, hbuf[:, c, :], Av[:, c:c + 1], rstd, ALU.mult, ALU.mult),
                 reads=["h:%d" % c, "rstd", "vecs"], writes=["nt1%d" % (c % 2)])
            if out_bf:
                P.op("act", lambda e, c=c, n1=n1: e.activation(ubuf[:, c, :], n1, AF.Identity, bias=Bv[:, c:c + 1]),
                     reads=["nt1%d" % (c % 2), "vecs"], writes=["u:%d" % c])
            else:
                P.op("act", lambda e, c=c, n1=n1: e.activation(hbuf[:, c, :], n1, AF.Identity, bias=Bv[:, c:c + 1]),
                     reads=["nt1%d" % (c % 2), "vecs"], writes=["h:%d" % c])

    UALL = ["u:%d" % c for c in range(NCH)]

    def proj_fm(ws, wkey, k, psbank):
        def mm(e):
            ins = None
            for kc in range(NCH):
                ins = e.matmul(PB(psbank), ws[:, k * 2048 + kc * 128:k * 2048 + (kc + 1) * 128], ubuf[:, kc, :],
                               start=(kc == 0), stop=(kc == NCH - 1))
            return ins
        P.op("pe", mm, reads=[wkey] + UALL, writes=["pb%d" % psbank])

    def ffn_phase(l, which, src, srckey, dst, dstkey):
        A.reset("f_")
        abuf = A.alloc("a", FC * T * 2, BF16).rearrange("p (c t) -> p c t", t=T)
        sq = [A.alloc("sq%d" % i, T * 2, BF16) for i in range(4)]
        rstd = A.alloc("rstd", T * 4, F32)
        nt1 = [A.alloc("nt1%d" % i, T * 4, F32) for i in range(2)]
        st1 = [A.alloc("st1%d" % i, T * 4, F32) for i in range(2)]
        Av, Bv, Gv = vec(l, which * 6), vec(l, which * 6 + 1), vec(l, which * 6 + 2)
        bbase = which * cfg.NBF
        for t in range(NTILE):
            load_tile(src, t, srckey)
            rmsnorm(sq, rstd, nt1, Av, Bv)
            n = 0
            for s2 in range(cfg.NB13):
                for rk in range(NCORE):
                    chunks = [(k, (2 * s2 + k) * 8 + rk) for k in range(2) if (2 * s2 + k) * 8 + rk < FC]
                    if not chunks:
                        continue
                    ws, wkey = ring_load(l, wblock(l, rk, bbase + s2))
                    for k, c in chunks:
                        p1, p3 = n % 2, 2 + n % 2
                        n += 1
                        def mm(e, ws=ws, k=k, p1=p1, p3=p3):
                            ins = None
                            for kc in range(NCH):
                                ins = e.matmul(PB(p1), ws[:, k * 4096 + kc * 128:k * 4096 + (kc + 1) * 128], ubuf[:, kc, :],
                                               start=(kc == 0), stop=(kc == NCH - 1))
                            for kc in range(NCH):
                                ins = e.matmul(PB(p3), ws[:, k * 4096 + 2048 + kc * 128:k * 4096 + 2048 + (kc + 1) * 128], ubuf[:, kc, :],
                                               start=(kc == 0), stop=(kc == NCH - 1))
                            return ins
                        P.op("pe", mm, reads=[wkey] + UALL, writes=["pb%d" % p1, "pb%d" % p3])
                        s1 = st1[n % 2]
                        P.op("act", lambda e, s1=s1, p1=p1: e.activation(s1, PB(p1), AF.Silu), reads=["pb%d" % p1], writes=["st1%d" % (n % 2)])
                        P.op("dve", lambda e, s1=s1, p3=p3, c=c: e.tensor_tensor(abuf[:, c, :], PB(p3), s1, ALU.mult),
                             reads=["pb%d" % p3, "st1%d" % (n % 2)], writes=["a:%d" % c])
            AALL = ["a:%d" % c for c in range(FC)]
            for dc in range(NCH):
                ws, wkey = ring_load(l, wblock(l, dc % 8, bbase + cfg.NB13 + dc // 8, FC * 128), FC * 128)
                py = 4 + dc % 2
                def mm2(e, ws=ws, py=py):
                    ins = None
                    for fc in range(FC):
                        ins = e.matmul(PB(py), ws[:, fc * 128:(fc + 1) * 128], abuf[:, fc, :], start=(fc == 0), stop=(fc == FC - 1))
                    return ins
                P.op("pe", mm2, reads=[wkey] + AALL, writes=["pb%d" % py])
                P.op("dve", lambda e, dc=dc, py=py: e.scalar_tensor_tensor(hbuf[:, dc, :], PB(py), Gv[:, dc:dc + 1], hbuf[:, dc, :], ALU.mult, ALU.add),
                     reads=["pb%d" % py, "h:%d" % dc, "vecs"], writes=["h:%d" % dc])
            store_tile(dst, t, dstkey)

    def mixer_phase(l, passB, src, srckey, dst, dstkey):
        A.reset("m_")
        xch = A.alloc("xch", SW * 4, F32)
        xin = [A.alloc("xin0", SW * 4, F32)] * 2
        hacc = A.alloc("hacc", 384 * 4, F32)
        A.seek(0)
        qa = A.alloc("qa", 8 * T * 2, BF16).rearrange("p (c t) -> p c t", t=T)
        qg = A.alloc("qg", 4 * T * 4, F32).rearrange("p (c t) -> p c t", t=T)
        kgT = A.alloc("kgT", 4 * T * 4, F32).rearrange("p (c t) -> p c t", t=T)
        sr = A.alloc("sr", 8 * T * 2, BF16).rearrange("p (c t) -> p c t", t=T)
        kgtok = A.alloc("kgtok", 4 * 512 * 4, F32).rearrange("p (b x) -> p b x", x=512)
        vgtok = A.alloc("vgtok", 4 * 1024 * 2, BF16).rearrange("p (b x) -> p b x", x=1024)
        ka = A.alloc("ka", 5 * 128 * 2, BF16)
        v01f = A.alloc("v01", 5 * 2 * 128 * 2, BF16)
        v01 = v01f.rearrange("p (b g d) -> p b g d", g=2, d=128)
        gateT = A.alloc("gateT", T * 4, F32)
        S = A.alloc("S", 1024 * 4, F32).rearrange("p (h e) -> p h e", e=256)
        Sbf = A.alloc("Sbf", 1024 * 2, BF16).rearrange("p (h e) -> p h e", e=256)
        bsum = A.alloc("bsum", 16, F32)
        e2 = A.alloc("e2", 512 * 4, F32)
        kttok = A.alloc("kttok", 512 * 2, BF16)
        ebT = A.alloc("ebT", 512 * 4, F32)
        qtT = A.alloc("qtT", 512 * 2, BF16).rearrange("p (h t) -> p h t", t=128)
        ktT = A.alloc("ktT", 512 * 2, BF16).rearrange("p (h t) -> p h t", t=128)
        atm = A.alloc("atm", 512 * 2, BF16).rearrange("p (h t) -> p h t", t=128)
        gsq = [A.alloc("gsq%d" % i, 128 * 4, F32) for i in range(2)]
        o1 = A.off
        e1 = A.alloc("e1", 512 * 4, F32)
        A.seek(o1)
        gtmp = A.alloc("gtmp", 512 * 4, F32)
        o2 = A.off
        enbtok = A.alloc("enbtok", 512 * 4, F32)
        A.seek(o2)
        sden = A.alloc("sden", 512 * 4, F32)
        o3 = A.off
        enbT = A.alloc("enbT", 512 * 4, F32)
        A.seek(o3)
        grstd = A.alloc("grstd", 512 * 4, F32)
        o4 = A.off
        sq = [A.alloc("sq%d" % i, T * 2, BF16) for i in range(4)]
        rstd = A.alloc("rstd", T * 4, F32)
        nt1 = [A.alloc("nt1%d" % i, T * 4, F32) for i in range(2)]
        A.seek(o4)
        sexp = [A.alloc("sexp%d" % i, 512 * 4, F32) for i in range(2)]
        spT = [A.alloc("spT%d" % i, 512 * 2, BF16) for i in range(2)]
        stmp = [A.alloc("stmp%d" % i, 256 * 4, F32) for i in range(2)]
        Av, Bv, Gv = vec(l, 3), vec(l, 4), vec(l, 5)
        fmb = 2 * cfg.NBF
        KA_ALL = ["ka:%d" % i for i in range(5)]

        P.op("pool", lambda e: e.memset(v01f, 0.0), writes=["v01:%d" % i for i in range(5)])
        P.op("pool", lambda e: e.memset(gateT[0:32, :], 1.0), writes=["gateT"])
        if not passB:
            P.op("pool", lambda e: e.memset(S, 0.0), writes=["S"])
            P.op("pool", lambda e: e.memset(bsum, 0.0), writes=["bsum"])
            P.op("pool", lambda e: e.memset(ka[:, 0:128], 0.0), writes=["ka:0"])
        else:
            P.op("pool", lambda e: e.memset(S, 0.0), writes=["S"])
            P.op("pool", lambda e: e.memset(hacc, 0.0), writes=["hacc"])
            for i in range(0 if cfg.solo else NCORE):
                xi = xin[i % 2]
                xk = "xin0"
                P.dma("sp", xi, stg[l].ap()[i * 128:(i + 1) * 128, :], reads=["stg%d" % l], writes=[xk], sem="xin0")
                P.op("dve", lambda e, xi=xi: e.tensor_scalar(xi[:, 1024:1028], xi[:, 1024:1028], -1.0, None, ALU.add), reads=[xk], writes=[xk])
                for h in range(GH):
                    st = stmp[h % 2]
                    sk = "stmp%d" % (h % 2)
                    P.op("dve", lambda e, xi=xi, h=h, st=st: e.scalar_tensor_tensor(st, S[:, h, :], xi[:, 1024 + h:1025 + h], xi[:, h * 256:(h + 1) * 256], ALU.mult, ALU.add),
                         reads=["S", xk], writes=[sk])
                    P.op("dve", lambda e, i=i, h=h, st=st: e.scalar_tensor_tensor(S[:, h, :], st, info[:, 8 + i:9 + i], S[:, h, :], ALU.mult, ALU.add),
                         reads=[sk, "info", "S"], writes=["S"])
                P.op("dve", lambda e, i=i, xi=xi: e.scalar_tensor_tensor(hacc, xi[:, 1032:1416], info[:, 16 + i:17 + i], hacc, ALU.mult, ALU.add),
                     reads=[xk, "info", "hacc"], writes=["hacc"])
            P.op("act", lambda e: e.copy(Sbf, S), reads=["S"], writes=["Sbf"])
            P.op("dve", lambda e: e.tensor_copy(ka[:, 0:128], hacc[:, 0:128]), reads=["hacc"], writes=["ka:0"])
            P.op("dve", lambda e: e.tensor_copy(v01[:, 0, 0, 0:64], hacc[:, 128:192]), reads=["hacc"], writes=["v01:0"])
            P.op("dve", lambda e: e.tensor_copy(v01[:, 0, 1, 64:128], hacc[:, 320:384]), reads=["hacc", "v01:0"], writes=["v01:0"])

        for t in range(NTILE):
            load_tile(src, t, srckey)
            rmsnorm(sq, rstd, nt1, Av, Bv)
            pj = 0
            for gi in range(7):
                cis = [gi * 4 + k for k in range(4) if gi * 4 + k < NFM]
                if not passB:
                    cis = [ci for ci in cis if ci in (8, 25)]
                if not cis:
                    continue
                ws, wkey = ring_load(l, wblock(l, gi, fmb))
                for ci in cis:
                    k = ci - gi * 4
                    bank = pj % 4
                    pj += 1
                    proj_fm(ws, wkey, k, bank)
                    bk = "pb%d" % bank
                    if ci < 8:
                        P.op("act", lambda e, ci=ci, bank=bank: e.activation(qa[:, ci, :], PB(bank), AF.Identity, scale=HD ** -0.5), reads=[bk], writes=["qa"])
                    elif ci == 8:
                        P.op("dve", lambda e, bank=bank: e.tensor_copy(ka[:, 128:640], PB(bank)), reads=[bk], writes=KA_ALL[1:])
                    elif ci < 13:
                        P.op("dve", lambda e, ci=ci, bank=bank: e.tensor_copy(qg[:, ci - 9, :], PB(bank)), reads=[bk], writes=["qg"])
                    elif ci < 17:
                        P.op("dve", lambda e, ci=ci, bank=bank: e.tensor_copy(kgT[:, ci - 13, :], PB(bank)), reads=[bk], writes=["kgT"])
                    elif ci < 25:
                        P.op("act", lambda e, ci=ci, bank=bank: e.activation(sr[:, ci - 17, :], PB(bank), AF.Silu), reads=[bk], writes=["sr"])
                    else:
                        P.op("dve", lambda e, bank=bank: e.tensor_copy(gateT[0:16, :], pb[bank][0:16, :]), reads=[bk, "gateT"], writes=["gateT"])
            for gj in range(4):
                ws, wkey = ring_load(l, wblock(l, gj, fmb + 1))
                for bi in range(4):
                    bank = pj % 4
                    pj += 1
                    def mm(e, ws=ws, bi=bi, bank=bank):
                        ins = None
                        for kc in range(NCH):
                            ins = e.matmul(PB(bank), ubuf[:, kc, bi * 128:(bi + 1) * 128], ws[:, kc * 512:(kc + 1) * 512],
                                           start=(kc == 0), stop=(kc == NCH - 1))
                        return ins
                    P.op("pe", mm, reads=[wkey] + UALL, writes=["pb%d" % bank])
                    bk = "pb%d" % bank
                    if gj < 2:
                        P.op("act", lambda e, gj=gj, bi=bi, bank=bank: e.copy(vgtok[:, bi, gj * 512:(gj + 1) * 512], PB(bank)), reads=[bk], writes=["vgtok:%d" % bi])
                    elif gj == 2:
                        P.op("dve", lambda e, bi=bi, bank=bank: e.tensor_copy(kgtok[:, bi, :], PB(bank)), reads=[bk], writes=["kgtok:%d" % bi])
                    else:
                        P.op("dve", lambda e, bi=bi, bank=bank: e.tensor_copy(v01[:, bi + 1, 0, 0:64], pb[bank][:, 0:64]), reads=[bk], writes=["v01:%d" % (bi + 1)])
                        P.op("dve", lambda e, bi=bi, bank=bank: e.tensor_copy(v01[:, bi + 1, 1, 64:128], pb[bank][:, 64:128]), reads=[bk, "v01:%d" % (bi + 1)], writes=["v01:%d" % (bi + 1)])

            for bi in range(4):
                bs = slice(bi * 128, (bi + 1) * 128)
                vgk = "vgtok:%d" % bi
                P.op("pe", lambda e, bs=bs: e.matmul(PB(4), gateT[0:17, bs], gw2s[0:17, l * 512:(l + 1) * 512], start=True, stop=True),
                     reads=["gateT", "gw2s", "gbs"], writes=["pb4"])
                P.op("act", lambda e: e.activation(e1, PB(4), AF.Exp, scale=-1.0), reads=["pb4"], writes=["e1"])
                P.op("act", lambda e: e.activation(e2, e1, AF.Ln, bias=1.0), reads=["e1"], writes=["e2"])
                P.op("pe", lambda e: e.matmul(PB(5), umat[:], e2, start=True, stop=True), reads=["umat", "e2"], writes=["pb5"])
                def mmb(e):
                    ins = None
                    for h in range(GH):
                        ins = e.matmul(PB3(7)[:, h, :], e2[:, h * 128:(h + 1) * 128], umat[:], start=True, stop=True)
                    return ins
                P.op("pe", mmb, reads=["umat", "e2"], writes=["pb7"])
                P.op("act", lambda e: e.activation(enbtok, PB(5), AF.Exp, scale=-1.0), reads=["pb5"], writes=["enbtok"])
                P.op("dve", lambda e, bi=bi: e.tensor_tensor(kttok, kgtok[:, bi, :], enbtok, ALU.mult), reads=["kgtok:%d" % bi, "enbtok"], writes=["kttok"])
                P.op("act", lambda e: e.activation(ebT, PB(7), AF.Exp), reads=["pb7"], writes=["ebT"])
                ebT3 = ebT.rearrange("p (h t) -> p h t", t=128)
                if not passB:
                    P.op("dve", lambda e: e.tensor_tensor(bsum, bsum, PB3(7)[:, :, 63], ALU.add), reads=["pb7", "bsum"], writes=["bsum"])
                    P.op("dve", lambda e: e.tensor_tensor(bsum, bsum, PB3(7)[:, :, 127], ALU.add), reads=["pb7", "bsum"], writes=["bsum"])
                else:
                    P.op("act", lambda e: e.activation(enbT, PB(7), AF.Exp, scale=-1.0), reads=["pb7"], writes=["enbT"])
                    P.op("dve", lambda e, bs=bs: e.scalar_tensor_tensor(qtT, qg[:, :, bs], GDK ** -0.5, ebT3, ALU.mult, ALU.mult),
                         reads=["qg", "ebT"], writes=["qtT"])
                    P.op("dve", lambda e, bs=bs: e.tensor_tensor(ktT, kgT[:, :, bs], enbT.rearrange("p (h t) -> p h t", t=128), ALU.mult),
                         reads=["kgT", "enbT"], writes=["ktT"])
                    def mma(e):
                        ins = None
                        for h in range(GH):
                            ins = e.matmul(PB3(4)[:, h, :], ktT[:, h, :], qtT[:, h, :], start=True, stop=True)
                        return ins
                    P.op("pe", mma, reads=["ktT", "qtT"], writes=["pb4"])
                    P.op("dve", lambda e: e.tensor_tensor(atm, PB3(4), gmask[:].unsqueeze(1).broadcast_to([128, GH, 128]), ALU.mult),
                         reads=["pb4", "gmask"], writes=["atm"])
                for j in range(2):
                    ts = slice(j * 64, (j + 1) * 64)
                    for h in range(GH):
                        if passB:
                            def mmo(e, h=h, j=j, ts=ts, bi=bi):
                                ins = None
                                for eh in range(2):
                                    o_ap = PB3(eh)[:, h, ts]
                                    e.matmul(o_ap, vgtok[ts, bi, h * 256 + eh * 128:h * 256 + (eh + 1) * 128], atm[ts, h, ts], start=True, stop=False)
                                    ins = e.matmul(o_ap, Sbf[:, h, eh * 128:(eh + 1) * 128], qtT[:, h, ts], start=False, stop=True)
                                return ins
                            P.op("pe", mmo, reads=[vgk, "atm", "Sbf", "qtT"], writes=["pb0", "pb1"])
                        hk = (j * GH + h) % 2
                        pkv = pb[2][:, hk * 256:(hk + 1) * 256]
                        P.op("pe", lambda e, h=h, ts=ts, bi=bi, pkv=pkv: e.matmul(pkv, kttok[ts, h * 128:(h + 1) * 128], vgtok[ts, bi, h * 256:(h + 1) * 256], start=True, stop=True),
                             reads=["kttok", vgk], writes=["pb2:%d" % hk])
                        st = stmp[hk]
                        dcol = ebT[:, h * 128 + j * 64 + 63:h * 128 + j * 64 + 64]
                        P.op("dve", lambda e, h=h, st=st, pkv=pkv: e.tensor_tensor(st, pkv, S[:, h, :], ALU.add), reads=["pb2:%d" % hk, "S"], writes=["stmp%d" % hk])
                        P.op("dve", lambda e, h=h, st=st, dcol=dcol: e.tensor_scalar(S[:, h, :], st, dcol, None, ALU.mult), reads=["stmp%d" % hk, "ebT"], writes=["S"])
                        if passB:
                            P.op("act", lambda e, h=h, st=st, dcol=dcol: e.activation(Sbf[:, h, :], st, AF.Identity, scale=dcol), reads=["stmp%d" % hk, "ebT"], writes=["Sbf"])
                if passB:
                    for h in range(GH):
                        for eh in range(2):
                            gs_ = gsq[eh]
                            P.op("act", lambda e, h=h, eh=eh, gs_=gs_: e.activation(gs_, PB3(eh)[:, h, :], AF.Square), reads=["pb%d" % eh], writes=["gsq%d" % eh])
                            P.op("pe", lambda e, h=h, eh=eh, gs_=gs_: e.matmul(PB3(5)[:, h, :], ones256[:], gs_, start=(eh == 0), stop=(eh == 1)),
                                 reads=["gsq%d" % eh, "ones256"], writes=["pb5"])
                    P.op("act", lambda e: e.activation(grstd, PB(5), AF.Ln, bias=epsb[:, 0:1]), reads=["pb5", "epsb"], writes=["grstd"])
                    P.op("act", lambda e: e.activation(grstd, grstd, AF.Exp, scale=-0.5), reads=["grstd"], writes=["grstd"])
                    for eh in range(2):
                        P.op("dve", lambda e, eh=eh: e.scalar_tensor_tensor(gtmp, PB(eh), glans[:, l * 2 + eh:l * 2 + eh + 1], grstd, ALU.mult, ALU.mult),
                             reads=["pb%d" % eh, "glans", "grstd"], writes=["gtmp"])
                        o_ap = ubuf[:, 8:16, bs].rearrange("p (h two) t -> p h two t", two=2)[:, :, eh, :]
                        s_ap = sr[:, :, bs].rearrange("p (h two) t -> p h two t", two=2)[:, :, eh, :]
                        P.op("dve", lambda e, o_ap=o_ap, s_ap=s_ap: e.tensor_tensor(o_ap, gtmp.rearrange("p (h t) -> p h t", t=128), s_ap, ALU.mult),
                             reads=["gtmp", "sr"], writes=["u:%d" % c for c in range(8, 16)])
                    first = (t == 0 and bi == 0)
                    sn = 0
                    for cg in range(2):
                        pv_bank, den_bank = 4, 7
                        for g in range(2):
                            gp = slice(g * 64, (g + 1) * 64)
                            for pc in range(2):
                                sbk = 3 if sn % 2 == 0 else 6
                                sx, sp_ = sexp[sn % 2], spT[sn % 2]
                                sxk, spk = "sexp%d" % (sn % 2), "spT%d" % (sn % 2)
                                sn += 1
                                kb = bi + pc
                                P.op("pe", lambda e, gp=gp, kb=kb, cg=cg, bs=bs, sbk=sbk: e.matmul(PB3(sbk), ka[gp, kb * 128:(kb + 1) * 128], qa[gp, cg * 4:(cg + 1) * 4, bs], start=True, stop=True),
                                     reads=["ka:%d" % kb, "qa"], writes=["pb%d" % sbk])
                                P.op("act", lambda e, sx=sx, sbk=sbk: e.activation(sx, PB(sbk), AF.Exp), reads=["pb%d" % sbk], writes=[sxk])
                                eb = ebias[:, ((pc * 2 + g) * 8 + cg * 4) * 128:((pc * 2 + g) * 8 + cg * 4 + 4) * 128]
                                if first and pc == 0:
                                    P.op("dve", lambda e, sx=sx, sp_=sp_, eb=eb: e.scalar_tensor_tensor(sp_, sx, info[:, 2:3], eb, ALU.mult, ALU.mult),
                                         reads=[sxk, "ebias", "info"], writes=[spk])
                                else:
                                    P.op("dve", lambda e, sx=sx, sp_=sp_, eb=eb: e.tensor_tensor(sp_, sx, eb, ALU.mult), reads=[sxk, "ebias"], writes=[spk])
                                fst, lst = (g == 0 and pc == 0), (g == 1 and pc == 1)
                                def mmpv(e, kb=kb, g=g, sp_=sp_, fst=fst, lst=lst):
                                    e.matmul(PB(4), v01[:, kb, g, :], sp_, start=fst, stop=lst)
                                    return e.matmul(PB(7), oneg[:, g, :], sp_, start=fst, stop=lst)
                                P.op("pe", mmpv, reads=["v01:%d" % kb, spk, "oneg"], writes=["pb4", "pb7"])
                        for c4 in range(4):
                            P.op("act", lambda e, c4=c4, cg=cg: e.activation(sden[:, c4 * 128:(c4 + 1) * 128], PB3(7)[:, c4, :], AF.Identity,
                                                                               bias=esink[:, l * 8 + cg * 4 + c4:l * 8 + cg * 4 + c4 + 1]),
                                 reads=["pb7", "esink", "sden"], writes=["sden"])
                        P.op("dve", lambda e: e.reciprocal(sden, sden), reads=["sden"], writes=["sden"])
                        P.op("dve", lambda e, cg=cg, bs=bs: e.tensor_tensor(ubuf[:, cg * 4:(cg + 1) * 4, bs], PB3(4), sden.rearrange("p (c t) -> p c t", t=128), ALU.mult),
                             reads=["pb4", "sden"], writes=["u:%d" % c for c in range(cg * 4, cg * 4 + 4)])
            if t < NTILE - 1:
                P.op("pool", lambda e: e.tensor_copy(ka[:, 0:128], ka[:, 512:640]), reads=["ka:4"], writes=["ka:0"])
                P.op("pool", lambda e: e.tensor_copy(v01[:, 0, :, :], v01[:, 4, :, :]), reads=["v01:4"], writes=["v01:0"])
            if passB:
                for dcg in range(4):
                    ws, wkey = ring_load(l, wblock(l, dcg, fmb + 2))
                    for k in range(4):
                        dc = dcg * 4 + k
                        bank = pj % 4
                        pj += 1
                        proj_fm(ws, wkey, k, bank)
                        P.op("dve", lambda e, dc=dc, bank=bank: e.scalar_tensor_tensor(hbuf[:, dc, :], PB(bank), Gv[:, dc:dc + 1], hbuf[:, dc, :], ALU.mult, ALU.add),
                             reads=["pb%d" % bank, "h:%d" % dc, "vecs"], writes=["h:%d" % dc])
                store_tile(dst, t, dstkey)
        if not passB:
            P.op("dve", lambda e: e.tensor_copy(xch[:, 0:1024], S.rearrange("p h e -> p (h e)")), reads=["S"], writes=["xch"])
            P.op("act", lambda e: e.activation(xch[:, 1024:1028], bsum, AF.Exp), reads=["bsum", "xch"], writes=["xch"])
            P.op("dve", lambda e: e.tensor_copy(xch[:, 1032:1160], ka[:, 512:640]), reads=["ka:4", "xch"], writes=["xch"])
            P.op("dve", lambda e: e.tensor_copy(xch[:, 1160:1416], v01[:, 4, :, :].rearrange("p g d -> p (g d)")), reads=["v01:4", "xch"], writes=["xch"])
            P.dma("sp", stb[l].ap()[:, 0:1416], xch[:, 0:1416], reads=["xch"], writes=["stb%d" % l], sem="k14")
            P.cc(lambda e: e.collective_compute("AllGather", ALU.bypass, replica_groups=RG,
                                                ins=[stb[l].ap().opt()], outs=[stg[l].ap().opt()]),
                 reads=["stb%d" % l], writes=["stg%d" % l], sem="ccs%d" % l)

    def final_phase(src, srckey):
        A.reset("z_")
        sq = [A.alloc("sq%d" % i, T * 2, BF16) for i in range(4)]
        rstd = A.alloc("rstd", T * 4, F32)
        nt1 = [A.alloc("nt1%d" % i, T * 4, F32) for i in range(2)]
        for t in range(NTILE):
            load_tile(src, t, srckey)
            rmsnorm(sq, rstd, nt1, vec(L, 0), vec(L, 1), out_bf=False)
            store_tile(yT, t, "yT", sem="sty")

    for l in range(L):
        pass

    stage = getattr(cfg, "stage", 99)
    last_src, last_key = xT, "xT"
    for l in range(L):
        src, srckey = (xT, "xT") if l == 0 else (hscr, "hs")
        if stage >= 2:
            ffn_phase(l, 0, src, srckey, hscr, "hs")
            last_src, last_key = hscr, "hs"
        if stage >= 3 and not cfg.solo:
            mixer_phase(l, False, hscr, "hs", hscr, "hs")
        if stage >= 4:
            mixer_phase(l, True, hscr, "hs", hscr, "hs")
        if stage >= 5:
            ffn_phase(l, 1, hscr, "hs", hscr, "hs")
    final_phase(last_src, last_key)
    P.build(final_waits=["sty"])
    return nc


def _fm_cols():
    offs = np.cumsum([0, 1024, 128, 128, 512, 512, 1024, 1024, 16])
    qa0, ka0, va0, qg0, kg0, vg0, rg0, gt0 = offs[:8]
    out = []
    for c in range(8):
        out.append(np.concatenate([qa0 + c * 64 + np.arange(64), qa0 + (c + 8) * 64 + np.arange(64)]))
    out.append(ka0 + np.arange(128))
    for h in range(4):
        out.append(qg0 + h * 128 + np.arange(128))
    for h in range(4):
        out.append(kg0 + h * 128 + np.arange(128))
    for j in range(8):
        out.append(rg0 + j * 128 + np.arange(128))
    out.append(gt0 + np.arange(16))
    tm = [vg0 + np.arange(512), vg0 + 512 + np.arange(512), kg0 + np.arange(512), va0 + np.arange(128)]
    return out, tm


def _chunkT(w):
    n = w.shape[1]
    return np.ascontiguousarray(w.reshape(16, 128, n).transpose(1, 0, 2)).reshape(128, 16 * n)


def host_prepare(cfg, x, c, ada_w, ada_b, norm_ffn1, ffn1_w1, ffn1_w3, ffn1_w2, norm_mix, w_in,
                 gla_gate_w2, gla_gate_b, attn_sinks, gla_norm, w_out, norm_ffn2, ffn2_w1,
                 ffn2_w3, ffn2_w2, final_ada_w, final_ada_b, final_norm):
    L, NT, FC, NBLK = cfg.L, cfg.NT, cfg.FC, cfg.NBLK
    f32 = np.float32
    fmc, tmc = _fm_cols()
    B, SEQ, _ = x.shape
    NCU = cfg.ncore
    CPS = NCU // B
    NR = NCORE if cfg.solo else 1
    assert SEQ == CPS * NT
    in_maps = []
    cTh = np.zeros((128, 16, 32), f32)
    cTh[:, :, 0:B] = np.asarray(c, f32).reshape(B, 16, 128).transpose(2, 1, 0)
    cTh = cTh.reshape(128, 512)
    nw = [np.asarray(a, f32) for a in (norm_ffn1, norm_mix, norm_ffn2)]
    normw = np.zeros((128, (3 * L + 1) * 16), f32)
    for l in range(L):
        for s in range(3):
            normw[:, (l * 3 + s) * 16:(l * 3 + s + 1) * 16] = nw[s][l].reshape(16, 128).T
    normw[:, 3 * L * 16:] = np.asarray(final_norm, f32).reshape(16, 128).T
    gw2 = np.ascontiguousarray(np.asarray(gla_gate_w2, f32)[:L].transpose(1, 0, 2)).reshape(16, L * 512)
    gbias = np.asarray(gla_gate_b, f32)[:L].reshape(1, L * 512)
    sinkT = np.zeros((128, L * 8), f32)
    sk = np.asarray(attn_sinks, f32)
    for l in range(L):
        for cc_ in range(8):
            sinkT[0:64, l * 8 + cc_] = sk[l, cc_]
            sinkT[64:128, l * 8 + cc_] = sk[l, cc_ + 8]
    glan = np.zeros((128, L * 2), f32)
    gn = np.asarray(gla_norm, f32)
    for l in range(L):
        glan[:, l * 2] = gn[l, 0:128]
        glan[:, l * 2 + 1] = gn[l, 128:256]
    s_idx = np.arange(128)
    same = (s_idx[:, None] // 64) == (s_idx[None, :] // 64)
    le = s_idx[:, None] <= s_idx[None, :]
    gmask = (same & le).astype(f32)
    umat = gmask * f32(-1.0 / 16.0)
    ebias = np.zeros((128, 2, 2, 8, 128), f32)
    key = np.arange(128)[:, None].astype(np.float64)
    q = np.arange(128)[None, :].astype(np.float64)
    for g in range(2):
        for cc_ in range(8):
            head = cc_ + 8 * g
            slope = 2.0 ** (-8.0 * (head + 1) / 16.0)
            d_cur = q - key
            ebias[:, 1, g, cc_, :] = np.where(d_cur >= 0, np.exp(-slope * d_cur), 0.0)
            d_prev = q + 128 - key
            ebias[:, 0, g, cc_, :] = np.where(d_prev < 128, np.exp(-slope * d_prev), 0.0)
    ebias = ebias.reshape(128, 4096)
    rowperm = np.concatenate([np.concatenate([cc_ * 64 + np.arange(64), (cc_ + 8) * 64 + np.arange(64)]) for cc_ in range(8)]
                             + [1024 + np.arange(1024)])
    adaw_all = np.asarray(ada_w, f32)
    adab_all = np.asarray(ada_b, f32)
    faw = np.asarray(final_ada_w, f32)
    fab = np.asarray(final_ada_b, f32)
    ffw = [(np.asarray(ffn1_w1, f32), np.asarray(ffn1_w3, f32), np.asarray(ffn1_w2, f32)),
           (np.asarray(ffn2_w1, f32), np.asarray(ffn2_w3, f32), np.asarray(ffn2_w2, f32))]
    win = np.asarray(w_in, f32)
    wout = np.asarray(w_out, f32)
    xx = np.asarray(x, f32)
    def rank_weights(r):
        wsh = np.zeros((L, NBLK, 128, WB), f32)
        for l in range(L):
            for wi in range(2):
                w1, w3, w2 = ffw[wi][0][l], ffw[wi][1][l], ffw[wi][2][l]
                fb = wi * cfg.NBF
                for s2 in range(cfg.NB13):
                    for k in range(2):
                        cch = (2 * s2 + k) * 8 + r
                        if cch < FC:
                            wsh[l, fb + s2, :, k * 4096:k * 4096 + 2048] = _chunkT(w1[:, cch * 128:(cch + 1) * 128])
                            wsh[l, fb + s2, :, k * 4096 + 2048:(k + 1) * 4096] = _chunkT(w3[:, cch * 128:(cch + 1) * 128])
                for slot in range(2):
                    dc = slot * 8 + r
                    wsh[l, fb + cfg.NB13 + slot, :, 0:FC * 128] = np.ascontiguousarray(
                        w2[:, dc * 128:(dc + 1) * 128].reshape(FC, 128, 128).transpose(1, 0, 2)).reshape(128, FC * 128)
            fmb = 2 * cfg.NBF
            if r < 7:
                for k in range(4):
                    ci = r * 4 + k
                    if ci < NFM:
                        cols = fmc[ci]
                        wc = np.zeros((2048, 128), f32)
                        wc[:, 0:len(cols)] = win[l][:, cols]
                        wsh[l, fmb, :, k * 2048:(k + 1) * 2048] = _chunkT(wc)
            if r < 4:
                cols = tmc[r]
                wc = np.zeros((2048, 512), f32)
                wc[:, 0:len(cols)] = win[l][:, cols]
                wsh[l, fmb + 1] = _chunkT(wc)
                for k in range(4):
                    dc = r * 4 + k
                    wsh[l, fmb + 2, :, k * 2048:(k + 1) * 2048] = _chunkT(wout[l][rowperm][:, dc * 128:(dc + 1) * 128])
        return wsh

    def rank_ada(r):
        adaw = np.zeros((cfg.NADT, 128, 2048), f32)
        adab = np.zeros((128, cfg.NADT), f32)
        for l in range(L):
            for cl in range(NAD):
                gc = r * NAD + cl
                adaw[l * NAD + cl] = _chunkT(adaw_all[l][:, gc * 128:(gc + 1) * 128])
                adab[:, l * NAD + cl] = adab_all[l, gc * 128:(gc + 1) * 128]
        for cl in range(4):
            gc = r * 4 + cl
            adaw[L * NAD + cl] = _chunkT(faw[:, gc * 128:(gc + 1) * 128])
            adab[:, L * NAD + cl] = fab[gc * 128:(gc + 1) * 128]
        return adaw, adab

    if cfg.solo:
        rw = [rank_weights(r) for r in range(NCORE)]
        wsh_all = np.stack(rw, axis=1).reshape(L * NCORE * NBLK * 128, WB)
        del rw
        ra = [rank_ada(r) for r in range(NCORE)]
        adaw_allr = np.concatenate([a[0] for a in ra], axis=0).reshape(NCORE * cfg.NADT * 128, 2048)
        adab_allr = np.concatenate([a[1] for a in ra], axis=1)
        del ra
    for r in range(NCU):
        b, j = r // CPS, r % CPS
        m = {}
        m["xT"] = np.ascontiguousarray(xx[b, j * NT:(j + 1) * NT, :].T)
        if cfg.solo:
            m["wsh"], m["adaw"], m["adab"] = wsh_all, adaw_allr, adab_allr
        else:
            m["wsh"] = rank_weights(r).reshape(L * NBLK * 128, WB)
            aw, ab = rank_ada(r)
            m["adaw"] = aw.reshape(cfg.NADT * 128, 2048)
            m["adab"] = ab
        m["cT"] = cTh
        m["normw"] = normw
        m["gw2"] = gw2
        m["gbias"] = gbias
        m["sinkT"] = sinkT
        m["glan"] = glan
        ci_ = np.zeros((128, NINFO), f32)
        ci_[:, b] = 1.0
        ci_[:, 2] = 1.0 if j > 0 else 0.0
        for i in range(NCU):
            if i // CPS == b and i < r:
                ci_[:, 8 + i] = 1.0
            if i // CPS == b and i == r - 1:
                ci_[:, 16 + i] = 1.0
        m["cinfo"] = ci_
        m["ebias"] = ebias
        m["umat"] = umat
        m["gmask"] = gmask
        in_maps.append(m)
    return in_maps


def run(cfg, inputs):
    nc = build_program(cfg)
    in_maps = host_prepare(cfg, **inputs)
    res = run_bass_kernel_spmd(nc, in_maps, core_ids=list(range(cfg.ncore)))
    B, SEQ, _ = inputs["x"].shape
    CPS = cfg.ncore // B
    out = np.empty((B, SEQ, D), np.float32)
    for r in range(cfg.ncore):
        b, j = r // CPS, r % CPS
        out[b, j * cfg.NT:(j + 1) * cfg.NT, :] = res.results[r]["yT"].T
    return out


MODE = "solo"


def kernel(**inputs):
    if MODE == "solo":
        cfg = Cfg(L=4, NT=16384, FC=44, solo=True)
    else:
        cfg = Cfg(L=4, NT=4096, FC=44, solo=False)
    return run(cfg, inputs)
```
